# Optimizing a Trainium2 kernel written in Bass

```python
import jax, jax.numpy as jnp
from jax import lax
import numpy as np

D_MODEL = 4096
BATCH = 1
SEQ = 8192
DEPTH = 4

CHUNK = 64
N_MIXERS = 2
CONV_KERNEL = 31
SB_HEAD_DIM = 128
SB_HEADS = D_MODEL // SB_HEAD_DIM
Q_BLOCK = 128
D_FF = ((8 * D_MODEL // 3 + 255) // 256) * 256
FFN_CONV = 3
EPS = 1e-6
N_CONV_LAYERS = (DEPTH + 1) // 2
N_SB_LAYERS = DEPTH // 2

kernel_name = "hybrid_conformer_stickbreaking_convffn"


def rmsnorm(x, g):
    xf = x.astype(jnp.float32)
    y = xf * lax.rsqrt(jnp.mean(xf * xf, axis=-1, keepdims=True) + EPS)
    return (y * g.astype(jnp.float32)).astype(x.dtype)


def layernorm(x, g, b):
    xf = x.astype(jnp.float32)
    mu = jnp.mean(xf, axis=-1, keepdims=True)
    xc = xf - mu
    var = jnp.mean(xc * xc, axis=-1, keepdims=True)
    y = xc * lax.rsqrt(var + EPS) * g.astype(jnp.float32) + b.astype(jnp.float32)
    return y.astype(x.dtype)


def causal_dwconv(x, w, b):
    width = w.shape[0]
    y = lax.conv_general_dilated(
        x, w[:, None, :].astype(x.dtype), window_strides=(1,), padding=[(width - 1, 0)],
        dimension_numbers=("NWC", "WIO", "NWC"), feature_group_count=x.shape[-1])
    return y + b


def conformer_conv(h, w_pw1, b_pw1, w_dw, b_dw, ln_g, ln_b, w_pw2, b_pw2):
    u = h @ w_pw1 + b_pw1
    a, gate = jnp.split(u, 2, axis=-1)
    u = a * jax.nn.sigmoid(gate)
    u = causal_dwconv(u, w_dw, b_dw)
    u = jax.nn.silu(layernorm(u, ln_g, ln_b))
    return u @ w_pw2 + b_pw2


def stick_breaking_attention(h, w_qkv, w_o):
    bsz, seq, _ = h.shape
    qkv = (h @ w_qkv).reshape(bsz, seq, 3, SB_HEADS, SB_HEAD_DIM)
    q = jnp.transpose(qkv[:, :, 0], (0, 2, 1, 3)).astype(jnp.float32)
    k = jnp.transpose(qkv[:, :, 1], (0, 2, 1, 3)).astype(jnp.float32)
    v = jnp.transpose(qkv[:, :, 2], (0, 2, 1, 3)).astype(jnp.float32)
    scale = SB_HEAD_DIM ** -0.5
    outs = []
    for blk in range(seq // Q_BLOCK):
        q0 = blk * Q_BLOCK
        k_end = q0 + Q_BLOCK
        qb = q[:, :, q0:k_end]
        kb = k[:, :, :k_end]
        vb = v[:, :, :k_end]
        z = jnp.einsum("bhqd,bhkd->bhqk", qb, kb) * scale
        t_idx = q0 + jnp.arange(Q_BLOCK)[:, None]
        s_idx = jnp.arange(k_end)[None, :]
        mask = s_idx < t_idx
        log_1m_beta = jnp.where(mask, jax.nn.log_sigmoid(-z), 0.0)
        later = lax.cumsum(log_1m_beta, axis=3, reverse=True) - log_1m_beta
        log_a = jax.nn.log_sigmoid(z) + later
        att = jnp.where(mask, jnp.exp(log_a), 0.0)
        outs.append(jnp.einsum("bhqk,bhkd->bhqd", att, vb))
    o = jnp.concatenate(outs, axis=2)
    o = jnp.transpose(o, (0, 2, 1, 3)).reshape(bsz, seq, SB_HEADS * SB_HEAD_DIM).astype(h.dtype)
    return o @ w_o


def conv_ffn(h, w_gate, w_up, w_dw, b_dw, w_down):
    g = causal_dwconv(h @ w_gate, w_dw, b_dw)
    u = h @ w_up
    return (jax.nn.silu(g) * u) @ w_down


def setup_inputs(seed: int = 0) -> dict:
    key = jax.random.key(seed)
    ks = jax.random.split(key, 24)
    f32 = jnp.float32
    D, F = D_MODEL, D_FF
    Lc, Ls, L = N_CONV_LAYERS, N_SB_LAYERS, DEPTH

    def nrm(k, shape, scale):
        return jax.random.normal(k, shape, f32) * scale

    return {
        "x": nrm(ks[0], (BATCH, SEQ, D), 1.0),
        "norm_mix": 1.0 + nrm(ks[1], (L, D), 0.02),
        "norm_ffn": 1.0 + nrm(ks[2], (L, D), 0.02),
        "final_norm": 1.0 + nrm(ks[3], (D,), 0.02),
        "cv_w_pw1": nrm(ks[4], (Lc, D, 2 * D), D ** -0.5),
        "cv_b_pw1": nrm(ks[5], (Lc, 2 * D), 0.02),
        "cv_w_dw": nrm(ks[6], (Lc, CONV_KERNEL, D), CONV_KERNEL ** -0.5),
        "cv_b_dw": nrm(ks[7], (Lc, D), 0.02),
        "cv_ln_g": 1.0 + nrm(ks[8], (Lc, D), 0.02),
        "cv_ln_b": nrm(ks[9], (Lc, D), 0.02),
        "cv_w_pw2": nrm(ks[10], (Lc, D, D), D ** -0.5),
        "cv_b_pw2": nrm(ks[11], (Lc, D), 0.02),
        "sb_w_qkv": nrm(ks[12], (Ls, D, 3 * D), D ** -0.5),
        "sb_w_o": nrm(ks[13], (Ls, D, D), D ** -0.5),
        "ff_w_gate": nrm(ks[14], (L, D, F), D ** -0.5),
        "ff_w_up": nrm(ks[15], (L, D, F), D ** -0.5),
        "ff_w_dw": nrm(ks[16], (L, FFN_CONV, F), FFN_CONV ** -0.5),
        "ff_b_dw": nrm(ks[17], (L, F), 0.02),
        "ff_w_down": nrm(ks[18], (L, F, D), F ** -0.5),
    }


def reference(x, norm_mix, norm_ffn, final_norm,
              cv_w_pw1, cv_b_pw1, cv_w_dw, cv_b_dw, cv_ln_g, cv_ln_b, cv_w_pw2, cv_b_pw2,
              sb_w_qkv, sb_w_o,
              ff_w_gate, ff_w_up, ff_w_dw, ff_b_dw, ff_w_down):
    for i in range(DEPTH):
        h = rmsnorm(x, norm_mix[i])
        j = i // N_MIXERS
        if i % N_MIXERS == 0:
            mix = conformer_conv(h, cv_w_pw1[j], cv_b_pw1[j], cv_w_dw[j], cv_b_dw[j],
                                 cv_ln_g[j], cv_ln_b[j], cv_w_pw2[j], cv_b_pw2[j])
        else:
            mix = stick_breaking_attention(h, sb_w_qkv[j], sb_w_o[j])
        x = x + mix
        h = rmsnorm(x, norm_ffn[i])
        x = x + conv_ffn(h, ff_w_gate[i], ff_w_up[i], ff_w_dw[i], ff_b_dw[i], ff_w_down[i])
    return rmsnorm(x, final_norm)
```

```python
import numpy as np
import ml_dtypes
import concourse.bass as bass
import concourse.mybir as mybir
from concourse.bass_utils import run_bass_kernel_spmd

F32 = mybir.dt.float32
BF16 = mybir.dt.bfloat16
AF = mybir.ActivationFunctionType
ALU = mybir.AluOpType
NPBF = ml_dtypes.bfloat16

CFG = dict(D=4096, F=11008, S=8192, NC=8, CK=31)
MAIN = 512
HALO = 32
TW = MAIN + HALO
EPS = 1e-6
NSLAB = 5
SLABE = 4096
ENGS = ("pe", "act", "dve", "pool", "sp")


def _I(name, *a, **k):
    return (name, a, k)


class Prog:
    def __init__(self, nc, same_engine_sync=True):
        self.nc = nc
        self.q = {e: [] for e in ENGS}
        self.cnt = {e: 0 for e in ENGS}
        self.dmacnt = {}
        self.waited = {e: {} for e in ENGS}
        self.lastw = {}
        self.readers = {}
        self.same_engine_sync = same_engine_sync

    PSUM_RES = ("M", "H", "S0", "S1", "S2", "Z", "B", "O", "T", "MV")

    def op(self, eng, fn, reads=(), writes=(), dma=None):
        ps = [r for r in reads if (r if isinstance(r, str) else r[0]) in self.PSUM_RES]
        if ps:
            reads = [r for r in reads if r not in ps]
            writes = list(writes) + [r for r in ps if r not in writes]
        deps = {}

        def add(tok):
            if tok is not None and deps.get(tok[0], 0) < tok[1]:
                deps[tok[0]] = tok[1]

        for r in reads:
            add(self.lastw.get(r))
        for w in writes:
            add(self.lastw.get(w))
            for k, v in self.readers.get(w, {}).items():
                add((k, v))
        waits = []
        for k, v in deps.items():
            if k == eng and (eng == "pe" or not self.same_engine_sync):
                continue
            if self.waited[eng].get(k, 0) < v:
                waits.append((k, v))
                self.waited[eng][k] = v
        if dma is None:
            self.cnt[eng] += 1
            tok = (eng, self.cnt[eng])
        else:
            self.dmacnt[dma] = self.dmacnt.get(dma, 0) + 16
            tok = (dma, self.dmacnt[dma])
        self.q[eng].append((waits, fn, tok))
        for r in reads:
            d = self.readers.setdefault(r, {})
            if d.get(tok[0], 0) < tok[1]:
                d[tok[0]] = tok[1]
        for w in writes:
            self.lastw[w] = tok
            self.readers[w] = {}
        return tok

    def emit(self, out_keys):
        import contextlib
        nc = self.nc
        keys = sorted(set(ENGS) | set(self.dmacnt.keys()), key=str)
        final = [(k, v) for k, v in self.dmacnt.items() if k in out_keys]
        with contextlib.ExitStack() as st:
            sems = {k: st.enter_context(nc.semaphore("s_" + str(k).replace(" ", ""))) for k in keys}
            block = st.enter_context(nc.Block())

            def run(e, eng):
                for waits, fn, tok in self.q[e]:
                    for k, v in waits:
                        eng.wait_ge(sems[k], v)
                    ins = getattr(eng, fn[0])(*fn[1], **fn[2])
                    ins.then_inc(sems[tok[0]], 1 if tok[0] in ENGS else 16)
                if e == "sp":
                    for k, v in final:
                        eng.wait_ge(sems[k], v)

            block.tensor(lambda eng: run("pe", eng))
            block.scalar(lambda eng: run("act", eng))
            block.vector(lambda eng: run("dve", eng))
            block.gpsimd(lambda eng: run("pool", eng))
            block.sync(lambda eng: run("sp", eng))


class Bld:
    def __init__(self, cfg):
        self.cfg = cfg
        self.nc = bass.Bass("TRN2", target_bir_lowering=False)
        self.P = Prog(self.nc)
        self.D, self.F = cfg["D"], cfg["F"]
        self.KC, self.FC = self.D // 128, self.F // 128
        self.ring = None
        self.ring_i = 0
        self.out_keys = set()
        self._u = 0

    def din(self, name, shape, dt=F32):
        return self.nc.dram_tensor(name, list(shape), dt, kind="ExternalInput").ap()

    def dout(self, name, shape, dt=F32):
        return self.nc.dram_tensor(name, list(shape), dt, kind="ExternalOutput").ap()

    def sb(self, name, shape, dt):
        return self.nc.alloc_sbuf_tensor("sb_" + name, list(shape), dt)

    def psb(self, name):
        return self.nc.alloc_psum_tensor("ps_" + name, [128, 512], F32)

    def uid(self):
        self._u += 1
        return self._u

    def make_ring(self):
        self.ring = [self.sb(f"ring{i}", [128, SLABE], BF16) for i in range(NSLAB)]

    def load_slab(self, src):
        s = self.ring_i % NSLAB
        self.ring_i += 1
        slot = self.ring[s]
        a, b = src.shape[1], src.shape[2]
        assert a * b <= SLABE
        dst = slot[:, 0:a * b].rearrange("p (a b) -> p a b", b=b)
        self.P.op("pool", _I("dma_start", out=dst, in_=src), writes=[("slab", s)], dma=f"w{s}")
        return s, slot

    def wslab(self, W, c0):
        return self.load_slab(W[:, c0:c0 + 128].rearrange("(kc p) c -> p kc c", p=128))

    def load_const(self, dram_ap, tile, res, key):
        self.P.op("sp", _I("dma_start", out=tile[:], in_=dram_ap), writes=[res], dma=key)

    def proj(self, s, slot, h, hres, KC, m0, mps, mres, halo=None):
        P = self.P
        for kc in range(KC):
            lhsT = slot[:, kc * 128:(kc + 1) * 128]
            st, sp = (kc == 0), (kc == KC - 1)
            P.op("pe", _I("matmul",
                mps[:, 0:MAIN], lhsT, h[:, kc, m0:m0 + MAIN], start=st, stop=sp),
                reads=[("slab", s), (hres, kc)], writes=[mres])
            if halo is not None:
                hap, hr = halo
                P.op("pe", _I("matmul",
                    hap, lhsT, h[:, kc, 0:HALO], start=st, stop=sp),
                    reads=[("slab", s), (hres, kc)], writes=[hr])

    def rmsnorm(self, x, xres, out, ores, gcol, W, m0, S0, S2, ones, pp, T32, T16, out_is_x=False):
        P, KC, D = self.P, self.KC, self.D
        has_halo = m0 > 0
        for fc in range(KC):
            sq = T16[fc % 2]
            P.op("act", _I("activation", out=sq[:, 0:W], in_=x[:, fc, 0:W], func=AF.Square),
                 reads=[(xres, fc)], writes=[("T16", fc % 2)])
            st, sp = (fc == 0), (fc == KC - 1)
            P.op("pe", _I("matmul", S0[:, 0:MAIN], ones[:, :], sq[:, m0:m0 + MAIN], start=st, stop=sp),
                 reads=[("T16", fc % 2), "ones"], writes=["S0"])
            if has_halo:
                P.op("pe", _I("matmul", S2[:, 0:HALO], ones[:, :], sq[:, 0:HALO], start=st, stop=sp),
                     reads=[("T16", fc % 2), "ones"], writes=["S2"])
        ta, tr = T32[0], T32[1]
        P.op("act", _I("activation", out=ta[:, m0:m0 + MAIN], in_=S0[:, 0:MAIN], func=AF.Sqrt, bias=EPS, scale=1.0 / D),
             reads=["S0"], writes=[("T32", 0)])
        if has_halo:
            P.op("act", _I("activation", out=ta[:, 0:HALO], in_=S2[:, 0:HALO], func=AF.Sqrt, bias=EPS, scale=1.0 / D),
                 reads=["S2"], writes=[("T32", 0)])
        P.op("dve", _I("reciprocal", out=tr[:, 0:W], in_=ta[:, 0:W]), reads=[("T32", 0)], writes=[("T32", 1)])
        for fc in range(KC):
            P.op("dve", _I("scalar_tensor_tensor",
                out=out[:, fc, 0:W], in0=x[:, fc, 0:W], scalar=pp[:, gcol + fc:gcol + fc + 1], in1=tr[:, 0:W],
                op0=ALU.mult, op1=ALU.mult),
                reads=[(xres, fc), ("T32", 1), "pp"], writes=[(ores, fc)])

    def ffn(self, x, hB, pp, off, Wg, Wu, Wd, M, H, T32, ACTG, flagcol, S0, S2, ones, T16, dbg=None):
        P, KC, FC = self.P, self.KC, self.FC
        self.rmsnorm(x, "x", hB, "hB", off["nffn"], TW, HALO, S0, S2, ones, pp, T32, T16)
        P.op("dve", _I("tensor_scalar", out=hB[:, :, 0:HALO], in0=hB[:, :, 0:HALO], scalar1=flagcol, scalar2=None, op0=ALU.mult),
             reads=[("hB", k) for k in range(KC)] + ["flag"], writes=[("hB", k) for k in range(KC)])
        if dbg is not None:
            self.out_keys.add("dbgh")
            P.op("sp", _I("dma_start", out=dbg, in_=hB[:, :, :]), reads=[("hB", k) for k in range(KC)], writes=[("dbgh",)], dma="dbgh")
        G = 2
        mi = [0]

        def nextM():
            i = mi[0] % len(M)
            mi[0] += 1
            return i

        hslot = [0]
        groups = [list(range(g0, min(g0 + G, FC))) for g0 in range(0, FC, G)]
        for gi, grp in enumerate(groups):
            ag = ACTG[gi % 2]
            for li, fc in enumerate(grp):
                sg_, slg = self.wslab(Wg, fc * 128)
                mg = nextM()
                hs = hslot[0] % len(H)
                hslot[0] += 1
                hap = H[hs][:, 0:HALO]
                self.proj(sg_, slg, hB, "hB", KC, HALO, M[mg], ("M", mg), halo=(hap, ("H", hs)))
                su_, slu = self.wslab(Wu, fc * 128)
                mu = nextM()
                self.proj(su_, slu, hB, "hB", KC, HALO, M[mu], ("M", mu))
                pb, gb, sgt = T32[2 + (fc % 2) * 2], T32[3 + (fc % 2) * 2], T32[0 + (fc % 2)]
                rpb, rgb, rsg = ("T32", 2 + (fc % 2) * 2), ("T32", 3 + (fc % 2) * 2), ("T32", fc % 2)
                P.op("act", _I("activation", out=pb[:, HALO:TW], in_=M[mg][:, 0:MAIN], func=AF.Copy),
                     reads=[("M", mg)], writes=[rpb])
                P.op("act", _I("activation", out=pb[:, 0:HALO], in_=hap, func=AF.Copy),
                     reads=[("H", hs)], writes=[rpb])
                if dbg is not None and fc == 0:
                    self.out_keys.update(["dbgpb", "dbghp"])
                    P.op("sp", _I("dma_start", out=self.dbg_pb, in_=pb[:, :]), reads=[rpb], writes=[("dbgpb",)], dma="dbgpb")
                    P.op("act", _I("activation", out=self.hdbg[:, :], in_=hap, func=AF.Copy), reads=[("H", hs)], writes=["hdbg"])
                    P.op("sp", _I("dma_start", out=self.dbg_hp, in_=self.hdbg[:, :]), reads=["hdbg"], writes=[("dbghp",)], dma="dbghp")
                c0 = off["fdw"] + fc * 3
                P.op("dve", _I("tensor_scalar",
                    out=gb[:, 0:MAIN], in0=pb[:, HALO:TW], scalar1=pp[:, c0 + 2:c0 + 3],
                    scalar2=pp[:, off["fdb"] + fc:off["fdb"] + fc + 1], op0=ALU.mult, op1=ALU.add),
                    reads=[rpb, "pp"], writes=[rgb])
                for k in (1, 0):
                    P.op("dve", _I("scalar_tensor_tensor",
                        out=gb[:, 0:MAIN], in0=pb[:, HALO - 2 + k:HALO - 2 + k + MAIN], scalar=pp[:, c0 + k:c0 + k + 1], in1=gb[:, 0:MAIN],
                        op0=ALU.mult, op1=ALU.add), reads=[rpb, rgb, "pp"], writes=[rgb])
                P.op("act", _I("activation", out=sgt[:, 0:MAIN], in_=gb[:, 0:MAIN], func=AF.Silu),
                     reads=[rgb], writes=[rsg])
                P.op("dve", _I("tensor_tensor",
                    out=ag[:, li, :], in0=sgt[:, 0:MAIN], in1=M[mu][:, 0:MAIN], op=ALU.mult),
                    reads=[rsg, ("M", mu)], writes=[("ag", gi % 2, li)])
            slabs = [self.load_slab(Wd[fc * 128:(fc + 1) * 128, :].rearrange("p (a c) -> p a c", a=1)) for fc in grp]
            for dc in range(KC):
                md = nextM()
                for li, fc in enumerate(grp):
                    s_, sl_ = slabs[li]
                    P.op("pe", _I("matmul",
                        M[md][:, 0:MAIN], sl_[:, dc * 128:(dc + 1) * 128], ag[:, li, :], start=(li == 0), stop=(li == len(grp) - 1)),
                        reads=[("slab", s_), ("ag", gi % 2, li)], writes=[("M", md)])
                P.op("dve", _I("tensor_tensor",
                    out=x[:, dc, HALO:TW], in0=M[md][:, 0:MAIN], in1=x[:, dc, HALO:TW], op=ALU.add),
                    reads=[("M", md), ("x", dc)], writes=[("x", dc)])


def pp_layout(cfg, kind):
    KC, FC, CK = cfg["D"] // 128, cfg["F"] // 128, cfg["CK"]
    off, n = {}, 0
    items = [("nffn", KC), ("fdw", FC * 3), ("fdb", FC), ("fin", KC)]
    if kind == "A":
        items = [("nmix", KC), ("b1", 2 * KC), ("dww", KC * CK), ("dwb", KC), ("lng", KC), ("lnb", KC), ("b2", KC)] + items
    if kind == "B1":
        items = [("nmix", KC)]
    for k, w in items:
        off[k] = n
        n += w
    off["_n"] = n
    return off


def vecp(v):
    v = np.asarray(v, np.float32)
    return np.ascontiguousarray(v.reshape(-1, 128).T)


def common_tiles(b, npp):
    nc = b.nc
    t = {}
    t["pp"] = b.sb("pp", [128, npp], F32)
    t["ones"] = b.sb("ones", [128, 128], BF16)
    t["T32"] = [b.sb(f"t32_{i}", [128, TW + 32], F32) for i in range(6)]
    t["T16"] = [b.sb(f"t16_{i}", [128, TW + 32], BF16) for i in range(2)]
    return t


DBG = False


def build_AB3(cfg, kind, final=False):
    b = Bld(cfg)
    nc, P = b.nc, b.P
    D, F, KC, FC, CK = b.D, b.F, b.KC, b.FC, cfg["CK"]
    NT = cfg["S"] // cfg["NC"] // MAIN
    off = pp_layout(cfg, kind)
    xin = b.din("xin", [NT, D, TW])
    flag = b.din("flag", [128, NT])
    ppd = b.din("pp", [128, off["_n"]])
    onesd = b.din("ones", [128, 128], BF16)
    if kind == "A":
        W1 = b.din("w_pw1", [D, 2 * D])
        W2 = b.din("w_pw2", [D, D])
    else:
        oin = b.din("oin", [NT, D, TW], BF16)
        Wo = b.din("w_o", [D, D])
    Wg = b.din("w_gate", [D, F])
    Wu = b.din("w_up", [D, F])
    Wd = b.din("w_down", [F, D])
    xout = b.dout("xout", [NT, D, MAIN])
    if DBG:
        dbg_x = b.dout("dbg_x", [NT, 128, KC, TW])
        dbg_h = b.dout("dbg_h", [NT, 128, KC, TW], BF16)
        dbg_v = b.dout("dbg_v", [NT, 128, KC, TW], BF16)
        dbg_c = b.dout("dbg_c", [NT, 128, KC, TW], BF16)
        dbg_pb = b.dout("dbg_pb", [NT, 128, TW + 32])
        dbg_hp = b.dout("dbg_hp", [NT, 128, HALO])
        b.hdbg = b.sb("hdbg", [128, HALO], F32)

    t = common_tiles(b, off["_n"])
    pp, ones, T32, T16 = t["pp"], t["ones"], t["T32"], t["T16"]
    x = b.sb("x", [128, KC, TW], F32)
    hA = b.sb("hA", [128, KC, TW], BF16)
    hB = b.sb("hB", [128, KC, TW], BF16)
    flagt = b.sb("flagt", [128, NT], F32)
    ACTG = [b.sb(f"actg{i}", [128, 2, MAIN], BF16) for i in range(2)]
    b.make_ring()
    M = [b.psb(f"M{i}") for i in range(3)]
    H = [b.psb("H0"), b.psb("H1")]
    AH = [b.sb(f"ahalo{i}", [128, HALO], F32) for i in range(2)]
    S0, S1, S2 = b.psb("S0"), b.psb("S1"), b.psb("S2")

    b.load_const(ppd, pp, "pp", "c_pp")
    b.load_const(onesd, ones, "ones", "c_ones")
    b.load_const(flag, flagt, "flag", "c_flag")

    for ti in range(NT):
        flagcol = flagt[:, ti:ti + 1]
        xsrc = xin[ti].rearrange("(fc p) t -> p fc t", p=128)
        XS = 8 if KC >= 8 else KC
        for i in range(0, KC, XS):
            P.op("sp", _I("dma_start", out=x[:, i:i + XS, :], in_=xsrc[:, i:i + XS, :]),
                 writes=[("x", k) for k in range(i, i + XS)], dma=f"x{i}")
        if kind == "A":
            b.rmsnorm(x, "x", hA, "hA", off["nmix"], TW, HALO, S0, S2, ones, pp, T32, T16)
            mi = 0
            for oc in range(KC):
                sa, sla = b.wslab(W1, oc * 128)
                ma = mi % 3; mi += 1
                b.proj(sa, sla, hA, "hA", KC, HALO, M[ma], ("M", ma), halo=(H[0][:, 0:HALO], ("H", 0)))
                bg = pp[:, off["b1"] + KC + oc:off["b1"] + KC + oc + 1]
                ba = pp[:, off["b1"] + oc:off["b1"] + oc + 1]
                ah = AH[oc % 2]
                P.op("dve", _I("tensor_scalar", out=ah[:, :], in0=H[0][:, 0:HALO], scalar1=ba, scalar2=None, op0=ALU.add),
                     reads=[("H", 0), "pp"], writes=[("ah", oc % 2)])
                sg, slg = b.wslab(W1, D + oc * 128)
                mg = mi % 3; mi += 1
                b.proj(sg, slg, hA, "hA", KC, HALO, M[mg], ("M", mg), halo=(H[1][:, 0:HALO], ("H", 1)))
                sgt = T32[oc % 2]
                rsg = ("T32", oc % 2)
                P.op("act", _I("activation", out=sgt[:, HALO:TW], in_=M[mg][:, 0:MAIN], func=AF.Sigmoid, bias=bg),
                     reads=[("M", mg), "pp"], writes=[rsg])
                P.op("act", _I("activation", out=sgt[:, 0:HALO], in_=H[1][:, 0:HALO], func=AF.Sigmoid, bias=bg),
                     reads=[("H", 1), "pp"], writes=[rsg])
                P.op("dve", _I("scalar_tensor_tensor",
                    out=hB[:, oc, HALO:TW], in0=M[ma][:, 0:MAIN], scalar=ba, in1=sgt[:, HALO:TW], op0=ALU.add, op1=ALU.mult),
                    reads=[("M", ma), rsg, "pp"], writes=[("hB", oc)])
                P.op("dve", _I("tensor_tensor", out=hB[:, oc, 0:HALO], in0=ah[:, :], in1=sgt[:, 0:HALO], op=ALU.mult),
                    reads=[("ah", oc % 2), rsg], writes=[("hB", oc)])
                P.op("dve", _I("tensor_scalar", out=hB[:, oc, 0:HALO], in0=hB[:, oc, 0:HALO], scalar1=flagcol, scalar2=None, op0=ALU.mult),
                     reads=[("hB", oc), "flag"], writes=[("hB", oc)])
                acc = T32[2 + (oc % 2)]
                racc = ("T32", 2 + (oc % 2))
                j0 = HALO - 2
                n = TW - j0
                wc = off["dww"] + oc * CK
                sh0 = j0 - (CK - 1)
                P.op("dve", _I("tensor_scalar",
                    out=acc[:, j0:TW], in0=hB[:, oc, sh0:sh0 + n], scalar1=pp[:, wc:wc + 1],
                    scalar2=pp[:, off["dwb"] + oc:off["dwb"] + oc + 1], op0=ALU.mult, op1=ALU.add),
                    reads=[("hB", oc), "pp"], writes=[racc])
                for k in range(1, CK):
                    last = (k == CK - 1)
                    dst = hB[:, oc, j0:TW] if last else acc[:, j0:TW]
                    P.op("dve", _I("scalar_tensor_tensor",
                        out=dst, in0=hB[:, oc, sh0 + k:sh0 + k + n], scalar=pp[:, wc + k:wc + k + 1], in1=acc[:, j0:TW],
                        op0=ALU.mult, op1=ALU.add),
                        reads=[("hB", oc), racc, "pp"], writes=[("hB", oc)] if last else [racc])
                sq = T16[oc % 2]
                rsq = ("T16", oc % 2)
                P.op("act", _I("activation", out=sq[:, 0:HALO], in_=hB[:, oc, 0:HALO], func=AF.Copy),
                     reads=[("hB", oc)], writes=[rsq])
                P.op("act", _I("activation", out=sq[:, HALO:HALO + TW], in_=hB[:, oc, 0:TW], func=AF.Square),
                     reads=[("hB", oc)], writes=[rsq])
                st, sp = (oc == 0), (oc == KC - 1)
                P.op("pe", _I("matmul", S0[:, 0:MAIN], ones[:, :], hB[:, oc, HALO:TW], start=st, stop=sp),
                     reads=[("hB", oc), "ones"], writes=["S0"])
                P.op("pe", _I("matmul", S1[:, 0:MAIN], ones[:, :], sq[:, 2 * HALO:2 * HALO + MAIN], start=st, stop=sp),
                     reads=[rsq, "ones"], writes=["S1"])
                P.op("pe", _I("matmul", S2[:, 0:2 * HALO], ones[:, :], sq[:, 0:2 * HALO], start=st, stop=sp),
                     reads=[rsq, "ones"], writes=["S2"])
            if DBG:
                b.out_keys.add("dbgc")
                P.op("sp", _I("dma_start", out=dbg_c[ti], in_=hB[:, :, :]), reads=[("hB", k) for k in range(KC)], writes=[("dbgc", ti)], dma="dbgc")
            mean, msq, var, rstd, nmr = T32[0], T32[1], T32[2], T32[3], T32[4]
            for (src, c0, c1, so) in ((S0, HALO, TW, 0), (S2, 0, HALO, 0)):
                P.op("dve", _I("tensor_scalar",
                    out=mean[:, c0:c1], in0=src[:, so:so + (c1 - c0)], scalar1=1.0 / D, scalar2=None, op0=ALU.mult),
                    reads=["S0", "S2"], writes=[("T32", 0)])
            P.op("dve", _I("tensor_tensor", out=msq[:, 0:TW], in0=mean[:, 0:TW], in1=mean[:, 0:TW], op=ALU.mult),
                 reads=[("T32", 0)], writes=[("T32", 1)])
            for (src, c0, c1, so) in ((S1, HALO, TW, 0), (S2, 0, HALO, HALO)):
                P.op("dve", _I("scalar_tensor_tensor",
                    out=var[:, c0:c1], in0=src[:, so:so + (c1 - c0)], scalar=1.0 / D, in1=msq[:, c0:c1], op0=ALU.mult, op1=ALU.subtract),
                    reads=["S1", "S2", ("T32", 1)], writes=[("T32", 2)])
            P.op("act", _I("activation", out=var[:, 0:TW], in_=var[:, 0:TW], func=AF.Sqrt, bias=EPS),
                 reads=[("T32", 2)], writes=[("T32", 2)])
            P.op("dve", _I("reciprocal", out=rstd[:, 0:TW], in_=var[:, 0:TW]), reads=[("T32", 2)], writes=[("T32", 3)])
            P.op("dve", _I("scalar_tensor_tensor", out=nmr[:, 0:TW], in0=mean[:, 0:TW], scalar=-1.0, in1=rstd[:, 0:TW], op0=ALU.mult, op1=ALU.mult),
                 reads=[("T32", 0), ("T32", 3)], writes=[("T32", 4)])
            for oc in range(KC):
                tt = T32[5] if oc % 2 == 0 else T32[1]
                rtt = ("T32", 5) if oc % 2 == 0 else ("T32", 1)
                P.op("dve", _I("tensor_tensor", out=tt[:, 0:TW], in0=hB[:, oc, 0:TW], in1=rstd[:, 0:TW], op=ALU.mult),
                     reads=[("hB", oc), ("T32", 3)], writes=[rtt])
                P.op("dve", _I("tensor_tensor", out=tt[:, 0:TW], in0=tt[:, 0:TW], in1=nmr[:, 0:TW], op=ALU.add),
                     reads=[rtt, ("T32", 4)], writes=[rtt])
                P.op("act", _I("activation", out=hA[:, oc, 0:TW], in_=tt[:, 0:TW], func=AF.Silu,
                                                               bias=pp[:, off["lnb"] + oc:off["lnb"] + oc + 1],
                                                               scale=pp[:, off["lng"] + oc:off["lng"] + oc + 1]),
                     reads=[rtt, "pp"], writes=[("hA", oc)])
            Wmix, bcol = W2, off["b2"]
        else:
            osrc = oin[ti].rearrange("(fc p) t -> p fc t", p=128)
            P.op("sp", _I("dma_start", out=hA[:, :, :], in_=osrc), writes=[("hA", k) for k in range(KC)], dma="oin")
            Wmix, bcol = Wo, None
            mi = 0
        for oc in range(KC):
            hs = oc % 2
            s_, sl_ = b.wslab(Wmix, oc * 128)
            m_ = mi % 3; mi += 1
            hap = H[hs][:, 0:HALO]
            b.proj(s_, sl_, hA, "hA", KC, HALO, M[m_], ("M", m_), halo=(hap, ("H", hs)))
            for (src, c0, c1, rs) in ((M[m_][:, 0:MAIN], HALO, TW, ("M", m_)), (hap, 0, HALO, ("H", hs))):
                if bcol is not None:
                    P.op("dve", _I("scalar_tensor_tensor",
                        out=x[:, oc, c0:c1], in0=src, scalar=pp[:, bcol + oc:bcol + oc + 1], in1=x[:, oc, c0:c1], op0=ALU.add, op1=ALU.add),
                        reads=[rs, ("x", oc), "pp"], writes=[("x", oc)])
                else:
                    P.op("dve", _I("tensor_tensor",
                        out=x[:, oc, c0:c1], in0=src, in1=x[:, oc, c0:c1], op=ALU.add),
                        reads=[rs, ("x", oc)], writes=[("x", oc)])
        if DBG:
            b.dbg_pb, b.dbg_hp = dbg_pb[ti], dbg_hp[ti]
            b.out_keys.update(["dbgx", "dbgv"])
            P.op("sp", _I("dma_start", out=dbg_x[ti], in_=x[:, :, :]), reads=[("x", k) for k in range(KC)], writes=[("dbgx", ti)], dma="dbgx")
            P.op("sp", _I("dma_start", out=dbg_v[ti], in_=hA[:, :, :]), reads=[("hA", k) for k in range(KC)], writes=[("dbgv", ti)], dma="dbgv")
        b.ffn(x, hB, pp, off, Wg, Wu, Wd, M, H, T32, ACTG, flagcol, S0, S2, ones, T16, dbg=(dbg_h[ti] if DBG else None))
        if final:
            b.rmsnorm(x, "x", x, "x", off["fin"], TW, HALO, S0, S2, ones, pp, T32, T16)
        odst = xout[ti].rearrange("(fc p) t -> p fc t", p=128)
        XS = 8 if KC >= 8 else KC
        for i in range(0, KC, XS):
            key = f"o{i}"
            b.out_keys.add(key)
            P.op("sp", _I("dma_start", out=odst[:, i:i + XS, :], in_=x[:, i:i + XS, HALO:TW]),
                 reads=[("x", k) for k in range(i, i + XS)], writes=[("xout", ti, i)], dma=key)
    P.emit(b.out_keys)
    return nc


def build_B1(cfg):
    b = Bld(cfg)
    nc, P = b.nc, b.P
    D, KC = b.D, b.KC
    TPC = cfg["S"] // cfg["NC"]
    NT = TPC // MAIN
    off = pp_layout(cfg, "B1")
    xin = b.din("xin", [NT, D, MAIN])
    ppd = b.din("pp", [128, off["_n"]])
    onesd = b.din("ones", [128, 128], BF16)
    Wq = b.din("w_qkv", [D, 3 * D])
    qT = b.dout("qT", [D, TPC], BF16)
    kT = b.dout("kT", [D, TPC], BF16)
    vo = b.dout("v", [TPC, D], BF16)
    t = common_tiles(b, off["_n"])
    pp, ones, T32, T16 = t["pp"], t["ones"], t["T32"], t["T16"]
    x = b.sb("x", [128, KC, MAIN], F32)
    hA = b.sb("hA", [128, KC, MAIN], BF16)
    STG = [b.sb(f"stg{i}", [128, MAIN], BF16) for i in range(4)]
    b.make_ring()
    M = [b.psb(f"M{i}") for i in range(3)]
    MV = [b.psb(f"MV{i}") for i in range(4)]
    S0 = b.psb("S0")
    b.load_const(ppd, pp, "pp", "c_pp")
    b.load_const(onesd, ones, "ones", "c_ones")
    scale = 128 ** -0.5
    si = 0
    for ti in range(NT):
        xsrc = xin[ti].rearrange("(fc p) t -> p fc t", p=128)
        XS = 8 if KC >= 8 else KC
        for i in range(0, KC, XS):
            P.op("sp", _I("dma_start", out=x[:, i:i + XS, :], in_=xsrc[:, i:i + XS, :]),
                 writes=[("x", k) for k in range(i, i + XS)], dma=f"x{i}")
        b.rmsnorm(x, "x", hA, "hA", off["nmix"], MAIN, 0, S0, None, ones, pp, T32, T16)
        mi = 0
        for oc in range(2 * KC):
            s_, sl_ = b.wslab(Wq, oc * 128)
            m_ = mi % 3; mi += 1
            b.proj(s_, sl_, hA, "hA", KC, 0, M[m_], ("M", m_))
            sg = si % 4; si += 1
            key = f"so{sg}"
            b.out_keys.add(key)
            P.op("act", _I("activation", out=STG[sg][:, :], in_=M[m_][:, 0:MAIN], func=AF.Copy,
                                                                 scale=(scale if oc < KC else 1.0)),
                 reads=[("M", m_)], writes=[("stg", sg)])
            dst = (qT if oc < KC else kT)[(oc % KC) * 128:(oc % KC + 1) * 128, ti * MAIN:(ti + 1) * MAIN]
            P.op("sp", _I("dma_start", out=dst, in_=STG[sg][:, :]),
                 reads=[("stg", sg)], writes=[("qk", ti, oc)], dma=key)
        KS = min(8, KC)
        for j in range(D // 512):
            for ks in range(KC // KS):
                src = Wq[ks * KS * 128:(ks + 1) * KS * 128, 2 * D + j * 512:2 * D + (j + 1) * 512].rearrange("(kc p) c -> p kc c", p=128)
                s_, sl_ = b.load_slab(src)
                for tb in range(MAIN // 128):
                    for kc in range(KS):
                        kk = ks * KS + kc
                        P.op("pe", _I("matmul",
                            MV[tb][:, 0:512], hA[:, kk, tb * 128:(tb + 1) * 128], sl_[:, kc * 512:(kc + 1) * 512], start=(kk == 0), stop=(kk == KC - 1)),
                            reads=[("slab", s_), ("hA", kk)], writes=[("MV", tb)])
            for tb in range(MAIN // 128):
                sg = si % 4; si += 1
                key = f"so{sg}"
                b.out_keys.add(key)
                P.op("act", _I("activation", out=STG[sg][:, :], in_=MV[tb][:, 0:512], func=AF.Copy),
                     reads=[("MV", tb)], writes=[("stg", sg)])
                dst = vo[ti * MAIN + tb * 128:ti * MAIN + (tb + 1) * 128, j * 512:(j + 1) * 512]
                P.op("sp", _I("dma_start", out=dst, in_=STG[sg][:, :]),
                     reads=[("stg", sg)], writes=[("vout", ti, j, tb)], dma=key)
    P.emit(b.out_keys)
    return nc


def build_B2(cfg):
    b = Bld(cfg)
    nc, P = b.nc, b.P
    S = cfg["S"]
    HPC = (cfg["D"] // 128) // cfg["NC"]
    NQ = S // MAIN
    NB = S // 128
    qd = b.din("q", [HPC, 128, S], BF16)
    kd = b.din("k", [HPC, 128, S], BF16)
    vd = b.din("v", [HPC, S, 128], BF16)
    cd = b.din("consts", [128, 3 * 128 + 4 * MAIN], BF16)
    od = b.dout("oT", [HPC, 128, S], BF16)
    cst = b.sb("cst", [128, 3 * 128 + 4 * MAIN], BF16)
    ones, negU, negI = cst[:, 0:128], cst[:, 128:256], cst[:, 256:384]
    masks = [cst[:, 384 + j * MAIN:384 + (j + 1) * MAIN] for j in range(4)]
    qs = [b.sb(f"q{i}", [128, S], BF16) for i in range(2)]
    ks_ = [b.sb(f"k{i}", [128, S], BF16) for i in range(2)]
    vs = [b.sb(f"v{i}", [128, NB, 128], BF16) for i in range(2)]
    NE, NSP, NAT, NTS, NOS = 2, 4, 3, 2, 2
    Eb = [b.sb(f"E{i}", [128, MAIN], F32) for i in range(NE)]
    SPb = [b.sb(f"SP{i}", [128, MAIN], BF16) for i in range(NSP)]
    ATb = [b.sb(f"AT{i}", [128, MAIN], BF16) for i in range(NAT)]
    TSb = [b.sb(f"TS{i}", [128, MAIN], BF16) for i in range(NTS)]
    OSb = [b.sb(f"OS{i}", [128, MAIN], BF16) for i in range(NOS)]
    Z = [b.psb(f"Z{i}") for i in range(2)]
    B2 = [b.psb(f"B{i}") for i in range(2)]
    O = [b.psb(f"O{i}") for i in range(2)]
    T = b.psb("T")
    b.load_const(cd, cst, "cst", "c_cst")

    tiles = []
    for h in range(HPC):
        for qc in range(NQ):
            kbs = list(range(4 * qc + 3, -1, -1))
            for i, kb in enumerate(kbs):
                tiles.append(dict(h=h, qc=qc, kb=kb, i=i, last=(i == len(kbs) - 1), j=kb - 4 * qc))
    for n, tl in enumerate(tiles):
        tl["n"] = n

    def load_head(h):
        s = h % 2
        P.op("sp", _I("dma_start", out=qs[s][:, :], in_=qd[h]), writes=[("q", s)], dma=f"hq{s}")
        P.op("sp", _I("dma_start", out=ks_[s][:, :], in_=kd[h]), writes=[("k", s)], dma=f"hk{s}")
        P.op("sp", _I("dma_start", out=vs[s][:, :, :], in_=vd[h].rearrange("(nb p) d -> p nb d", p=128)),
             writes=[("v", s)], dma=f"hv{s}")

    def kq(tl):
        s = tl["h"] % 2
        return ks_[s][:, tl["kb"] * 128:(tl["kb"] + 1) * 128], qs[s][:, tl["qc"] * MAIN:(tl["qc"] + 1) * MAIN], s

    def S1(tl):
        n = tl["n"]
        if tl["qc"] == 0 and tl["i"] == 0:
            load_head(tl["h"])
        kap, qap, s = kq(tl)
        zi, ei, spi = n % 2, n % NE, n % NSP
        P.op("pe", _I("matmul", Z[zi][:, 0:MAIN], kap, qap, start=True, stop=True),
             reads=[("k", s), ("q", s)], writes=[("Z", zi)])
        P.op("act", _I("activation", out=Eb[ei][:, :], in_=Z[zi][:, 0:MAIN], func=AF.Exp),
             reads=[("Z", zi)], writes=[("E", ei)])
        P.op("act", _I("activation", out=SPb[spi][:, :], in_=Eb[ei][:, :], func=AF.Ln, bias=1.0),
             reads=[("E", ei)], writes=[("SP", spi)])
        if tl["j"] >= 0:
            mk = masks[tl["j"]]
            P.op("dve", _I("tensor_tensor", out=SPb[spi][:, :], in0=SPb[spi][:, :], in1=mk, op=ALU.mult),
                 reads=[("SP", spi), "cst"], writes=[("SP", spi)])

    def S3(tl):
        n = tl["n"]
        kap, qap, s = kq(tl)
        bi, spi, ati = n % 2, n % NSP, n % NAT
        first = tl["i"] == 0
        P.op("pe", _I("matmul", B2[bi][:, 0:MAIN], kap, qap, start=True, stop=False),
             reads=[("k", s), ("q", s)], writes=[("B", bi)])
        P.op("pe", _I("matmul", B2[bi][:, 0:MAIN], negU, SPb[spi][:, :], start=False, stop=first),
             reads=[("SP", spi), "cst"], writes=[("B", bi)])
        if not first:
            tsi = tl["i"] % NTS
            P.op("pe", _I("matmul", B2[bi][:, 0:MAIN], negI, TSb[tsi][:, :], start=False, stop=True),
                 reads=[("TS", tsi), "cst"], writes=[("B", bi)])
        P.op("act", _I("activation", out=ATb[ati][:, :], in_=B2[bi][:, 0:MAIN], func=AF.Exp),
             reads=[("B", bi)], writes=[("AT", ati)])
        if tl["j"] >= 0:
            mk = masks[tl["j"]]
            P.op("dve", _I("tensor_tensor", out=ATb[ati][:, :], in0=ATb[ati][:, :], in1=mk, op=ALU.mult),
                 reads=[("AT", ati), "cst"], writes=[("AT", ati)])

    def S2(tl):
        if tl["last"]:
            return
        n = tl["n"]
        spi = n % NSP
        P.op("pe", _I("matmul", T[:, 0:MAIN], ones, SPb[spi][:, :], start=(tl["i"] == 0), stop=True, skip_group_check=True),
             reads=[("SP", spi), "cst"], writes=["T"])
        tsi = (tl["i"] + 1) % NTS
        P.op("dve", _I("tensor_copy", out=TSb[tsi][:, :], in_=T[:, 0:MAIN]), reads=["T"], writes=[("TS", tsi)])

    def S4(tl):
        n = tl["n"]
        s = tl["h"] % 2
        ati = n % NAT
        oi = (tl["h"] * NQ + tl["qc"]) % 2
        P.op("pe", _I("matmul", O[oi][:, 0:MAIN], vs[s][:, tl["kb"], :], ATb[ati][:, :], start=(tl["i"] == 0), stop=tl["last"]),
             reads=[("v", s), ("AT", ati)], writes=[("O", oi)])
        if tl["last"]:
            osi = (tl["h"] * NQ + tl["qc"]) % NOS
            key = f"oo{osi}"
            b.out_keys.add(key)
            P.op("act", _I("activation", out=OSb[osi][:, :], in_=O[oi][:, 0:MAIN], func=AF.Copy),
                 reads=[("O", oi)], writes=[("OS", osi)])
            P.op("sp", _I("dma_start", out=od[tl["h"], :, tl["qc"] * MAIN:(tl["qc"] + 1) * MAIN], in_=OSb[osi][:, :]),
                 reads=[("OS", osi)], writes=[("oT", tl["h"], tl["qc"])], dma=key)

    N = len(tiles)
    for n in range(N + 2):
        if n < N:
            S1(tiles[n])
        if 0 <= n - 1 < N:
            S3(tiles[n - 1])
            S2(tiles[n - 1])
        if 0 <= n - 2 < N:
            S4(tiles[n - 2])
    P.emit(b.out_keys)
    return nc


_PROGS = {}


def _prog(cfg, kind, final=False):
    key = (tuple(sorted(cfg.items())), kind, final)
    if key not in _PROGS:
        if kind in ("A", "B3"):
            _PROGS[key] = build_AB3(cfg, kind, final)
        elif kind == "B1":
            _PROGS[key] = build_B1(cfg)
        else:
            _PROGS[key] = build_B2(cfg)
    return _PROGS[key]


def _run(nc, in_maps, ncores):
    res = run_bass_kernel_spmd(nc, in_maps, core_ids=list(range(ncores)))
    return res.results


def _tiles_halo(xT, cfg, dtype):
    D, S, NC = cfg["D"], cfg["S"], cfg["NC"]
    TPC = S // NC
    NT = TPC // MAIN
    pad = np.concatenate([np.zeros((D, HALO), dtype), xT.astype(dtype, copy=False)], axis=1)
    out, flags = [], []
    for c in range(NC):
        tl = [pad[:, c * TPC + t * MAIN: c * TPC + t * MAIN + TW] for t in range(NT)]
        out.append(np.ascontiguousarray(np.stack(tl, 0)))
        fl = np.ones((128, NT), np.float32)
        if c == 0:
            fl[:, 0] = 0.0
        flags.append(fl)
    return out, flags


def _untile(res, key, cfg):
    return np.concatenate([np.concatenate(list(r[key]), axis=1) for r in res], axis=1)


def _pack_pp(cfg, kind, parts):
    off = pp_layout(cfg, kind)
    pp = np.zeros((128, off["_n"]), np.float32)
    for k, v in parts.items():
        pp[:, off[k]:off[k] + v.shape[1]] = v
    return pp


def _b2_consts():
    c = np.zeros((128, 3 * 128 + 4 * MAIN), np.float32)
    c[:, 0:128] = 1.0
    jj, ss = np.meshgrid(np.arange(128), np.arange(128), indexing="ij")
    c[:, 128:256] = np.where(jj >= ss, -1.0, 0.0)
    c[:, 256:384] = -np.eye(128, dtype=np.float32)
    s = np.arange(128)[:, None]
    t = np.arange(MAIN)[None, :]
    for j in range(4):
        c[:, 384 + j * MAIN:384 + (j + 1) * MAIN] = (s + 128 * j < t).astype(np.float32)
    return c.astype(NPBF)


def kernel(x, norm_mix, norm_ffn, final_norm,
           cv_w_pw1, cv_b_pw1, cv_w_dw, cv_b_dw, cv_ln_g, cv_ln_b, cv_w_pw2, cv_b_pw2,
           sb_w_qkv, sb_w_o, ff_w_gate, ff_w_up, ff_w_dw, ff_b_dw, ff_w_down, cfg=None):
    cfg = dict(cfg or CFG)
    D, F, S, NC, CK = cfg["D"], cfg["F"], cfg["S"], cfg["NC"], cfg["CK"]
    KC, FC = D // 128, F // 128
    TPC = S // NC
    NT = TPC // MAIN
    HPC = KC // NC
    depth = norm_mix.shape[0]
    f32 = lambda a: np.ascontiguousarray(np.asarray(a, np.float32))
    ones = np.ones((128, 128), NPBF)
    xT = np.ascontiguousarray(f32(x)[0].T)

    def ffn_parts(i):
        return dict(nffn=vecp(norm_ffn[i]),
                    fdw=np.ascontiguousarray(f32(ff_w_dw[i]).T.reshape(FC, 128, 3).transpose(1, 0, 2).reshape(128, FC * 3)),
                    fdb=vecp(ff_b_dw[i]), fin=vecp(final_norm))

    for i in range(depth):
        j = i // 2
        final = (i == depth - 1)
        if i % 2 == 0:
            parts = ffn_parts(i)
            parts.update(nmix=vecp(norm_mix[i]), b1=vecp(cv_b_pw1[j]),
                         dww=np.ascontiguousarray(f32(cv_w_dw[j]).T.reshape(KC, 128, CK).transpose(1, 0, 2).reshape(128, KC * CK)),
                         dwb=vecp(cv_b_dw[j]), lng=vecp(cv_ln_g[j]), lnb=vecp(cv_ln_b[j]), b2=vecp(cv_b_pw2[j]))
            pp = _pack_pp(cfg, "A", parts)
            tl, fl = _tiles_halo(xT, cfg, np.float32)
            w = dict(w_pw1=f32(cv_w_pw1[j]), w_pw2=f32(cv_w_pw2[j]), w_gate=f32(ff_w_gate[i]),
                     w_up=f32(ff_w_up[i]), w_down=f32(ff_w_down[i]), pp=pp, ones=ones)
            res = _run(_prog(cfg, "A", final), [dict(w, xin=tl[c], flag=fl[c]) for c in range(NC)], NC)
            xT = _untile(res, "xout", cfg)
        else:
            pp1 = _pack_pp(cfg, "B1", dict(nmix=vecp(norm_mix[i])))
            wq = f32(sb_w_qkv[j])
            ins = []
            for c in range(NC):
                xm = np.ascontiguousarray(np.stack([xT[:, c * TPC + t * MAIN: c * TPC + (t + 1) * MAIN] for t in range(NT)], 0))
                ins.append(dict(xin=xm, pp=pp1, ones=ones, w_qkv=wq))
            r1 = _run(_prog(cfg, "B1"), ins, NC)
            qT = np.concatenate([r["qT"] for r in r1], axis=1)
            kT = np.concatenate([r["kT"] for r in r1], axis=1)
            v = np.concatenate([r["v"] for r in r1], axis=0)
            consts = _b2_consts()
            ins = []
            for c in range(NC):
                sl = slice(c * HPC * 128, (c + 1) * HPC * 128)
                ins.append(dict(q=np.ascontiguousarray(qT[sl].reshape(HPC, 128, S)),
                                k=np.ascontiguousarray(kT[sl].reshape(HPC, 128, S)),
                                v=np.ascontiguousarray(v[:, sl].reshape(S, HPC, 128).transpose(1, 0, 2)),
                                consts=consts))
            r2 = _run(_prog(cfg, "B2"), ins, NC)
            oT = np.concatenate([r["oT"].reshape(HPC * 128, S) for r in r2], axis=0)
            pp3 = _pack_pp(cfg, "B3", ffn_parts(i))
            tl, fl = _tiles_halo(xT, cfg, np.float32)
            ol, _ = _tiles_halo(oT, cfg, NPBF)
            w = dict(w_o=f32(sb_w_o[j]), w_gate=f32(ff_w_gate[i]), w_up=f32(ff_w_up[i]), w_down=f32(ff_w_down[i]),
                     pp=pp3, ones=ones)
            res = _run(_prog(cfg, "B3", final), [dict(w, xin=tl[c], oin=ol[c], flag=fl[c]) for c in range(NC)], NC)
            xT = _untile(res, "xout", cfg)
    return np.ascontiguousarray(xT.T)[None].astype(np.float32)
```
